# Optimizing a Trainium2 kernel written in Bass

```python
import math
import jax, jax.numpy as jnp
from jax import lax
import numpy as np

D_MODEL = 2048
BATCH = 1
SEQ = 16384
DEPTH = 2

CHUNK = 64
QBLK = 128
D_MIX = D_MODEL

A_HEADS = 6
A_DHALF = 64
A_DV = 2 * A_DHALF
A_WIDTH = A_HEADS * A_DV

B_WIDTH = 512
CONV_W = 3

C_HEADS = 6
C_NOPE = 64
C_ROPE = 32
C_DV = 128
C_WIDTH = C_HEADS * C_DV
C_Q_RANK = 512
C_KV_RANK = 256
ROPE_THETA = 10000.0

D_FF = 5632

ALPHA = (2.0 * DEPTH) ** 0.25
BETA = (8.0 * DEPTH) ** -0.25

LN_EPS = 1e-5
RMS_EPS = 1e-6
NEG = -1e30

A_Q_COLS = A_HEADS * 2 * A_DHALF
A_K_COLS = A_HEADS * 2 * A_DHALF
A_V_COLS = A_HEADS * A_DV
IN_SIZES = (A_Q_COLS, A_K_COLS, A_V_COLS, B_WIDTH, B_WIDTH, B_WIDTH, C_Q_RANK, C_KV_RANK, C_ROPE)
IN_SPLITS = (768, 1536, 2304, 2816, 3328, 3840, 4352, 4608)
P_IN = 4640

kernel_name = "hymba_style_diffattn_shortconv_mla_convffn_deepnorm"


def layer_norm(x, g, b):
    xf = x.astype(jnp.float32)
    mu = xf.mean(-1, keepdims=True)
    var = jnp.square(xf - mu).mean(-1, keepdims=True)
    return ((xf - mu) * lax.rsqrt(var + LN_EPS) * g.astype(jnp.float32) + b.astype(jnp.float32)).astype(x.dtype)


def rms_norm(x, g):
    xf = x.astype(jnp.float32)
    r = lax.rsqrt(jnp.mean(jnp.square(xf), -1, keepdims=True) + RMS_EPS)
    return (xf * r * g.astype(jnp.float32)).astype(x.dtype)


def causal_dwconv3(x, w):
    s = x.shape[1]
    xp = jnp.pad(x, ((0, 0), (CONV_W - 1, 0), (0, 0)))
    return sum(w[i] * xp[:, i:i + s] for i in range(CONV_W))


def rope(x, pos):
    d = x.shape[-1]
    inv_freq = ROPE_THETA ** (-jnp.arange(0, d, 2, dtype=jnp.float32) / d)
    ang = pos.astype(jnp.float32)[..., None] * inv_freq
    cos = jnp.cos(ang)[:, :, None, :]
    sin = jnp.sin(ang)[:, :, None, :]
    xf = x.astype(jnp.float32)
    x1, x2 = xf[..., : d // 2], xf[..., d // 2:]
    return jnp.concatenate([x1 * cos - x2 * sin, x1 * sin + x2 * cos], -1).astype(x.dtype)


def chunk_mask(q_idx, s):
    return (q_idx[:, None] // CHUNK) >= (jnp.arange(s)[None, :] // CHUNK)


def to_qblocks(a):
    b, h, s = a.shape[:3]
    return jnp.moveaxis(a.reshape((b, h, s // QBLK, QBLK) + a.shape[3:]), 2, 0)


def from_qblocks(a):
    nq, b, h, q, d = a.shape
    return jnp.moveaxis(a, 0, 2).reshape(b, h, nq * q, d)


def diff_attention(q, k, v, lam, pos, slopes):
    b, h, s = q.shape[:3]
    nq = s // QBLK
    scale = A_DHALF ** -0.5
    k1, k2 = k[..., 0, :], k[..., 1, :]
    pos_k = pos[:, None, None, :]
    qb = to_qblocks(q)
    pos_q = jnp.moveaxis(pos.reshape(b, nq, QBLK), 1, 0)
    idx_q = jnp.arange(s).reshape(nq, QBLK)

    def one(blk):
        qblk, pq, iq = blk
        dist = jnp.abs(pq[:, None, :, None] - pos_k).astype(jnp.float32)
        bias = -slopes[None, :, None, None] * dist
        mask = chunk_mask(iq, s)

        def probs(qh, kh):
            sc = jnp.einsum('bhqd,bhkd->bhqk', qh, kh).astype(jnp.float32) * scale + bias
            return jax.nn.softmax(jnp.where(mask, sc, NEG), axis=-1)

        p = probs(qblk[..., 0, :], k1) - lam * probs(qblk[..., 1, :], k2)
        return jnp.einsum('bhqk,bhkd->bhqd', p.astype(v.dtype), v)

    return from_qblocks(lax.map(one, (qb, pos_q, idx_q)))


def mla_attention(q, k, v):
    b, h, s = q.shape[:3]
    nq = s // QBLK
    scale = (C_NOPE + C_ROPE) ** -0.5
    qb = to_qblocks(q)
    idx_q = jnp.arange(s).reshape(nq, QBLK)

    def one(blk):
        qblk, iq = blk
        sc = jnp.einsum('bhqd,bhkd->bhqk', qblk, k).astype(jnp.float32) * scale
        p = jax.nn.softmax(jnp.where(chunk_mask(iq, s), sc, NEG), axis=-1)
        return jnp.einsum('bhqk,bhkd->bhqd', p.astype(v.dtype), v)

    return from_qblocks(lax.map(one, (qb, idx_q)))


def mixer(h, pos, lam_init, w_in, diff_lambda, diff_norm_g, conv_w,
          mla_q_norm_g, mla_kv_norm_g, w_uq, w_ukv, w_o):
    b, s, _ = h.shape
    proj = h @ w_in
    a_q, a_k, a_v, b_h, b_b, b_c, c_q, c_kv, c_kr = jnp.split(proj, IN_SPLITS, axis=-1)

    q = a_q.reshape(b, s, A_HEADS, 2, A_DHALF).transpose(0, 2, 1, 3, 4)
    k = a_k.reshape(b, s, A_HEADS, 2, A_DHALF).transpose(0, 2, 1, 3, 4)
    v = a_v.reshape(b, s, A_HEADS, A_DV).transpose(0, 2, 1, 3)
    lf = diff_lambda.astype(jnp.float32)
    lam = jnp.exp(jnp.sum(lf[0] * lf[1])) - jnp.exp(jnp.sum(lf[2] * lf[3])) + lam_init
    slopes = 2.0 ** (-8.0 * (jnp.arange(A_HEADS, dtype=jnp.float32) + 1.0) / A_HEADS)
    o_a = diff_attention(q, k, v, lam, pos, slopes)
    o_a = rms_norm(o_a, diff_norm_g) * (1.0 - lam_init)
    o_a = o_a.transpose(0, 2, 1, 3).reshape(b, s, A_WIDTH)

    o_b = b_b * causal_dwconv3(b_c * b_h, conv_w)

    cq = rms_norm(c_q, mla_q_norm_g)
    qc = (cq @ w_uq).reshape(b, s, C_HEADS, C_NOPE + C_ROPE)
    q_nope, q_pe = qc[..., :C_NOPE], rope(qc[..., C_NOPE:], pos)
    ckv = rms_norm(c_kv, mla_kv_norm_g)
    kv = (ckv @ w_ukv).reshape(b, s, C_HEADS, C_NOPE + C_DV)
    k_nope, v_c = kv[..., :C_NOPE], kv[..., C_NOPE:]
    k_pe = jnp.broadcast_to(rope(c_kr[:, :, None, :], pos), (b, s, C_HEADS, C_ROPE))
    qm = jnp.concatenate([q_nope, q_pe], -1).transpose(0, 2, 1, 3)
    km = jnp.concatenate([k_nope, k_pe], -1).transpose(0, 2, 1, 3)
    o_c = mla_attention(qm, km, v_c.transpose(0, 2, 1, 3))
    o_c = o_c.transpose(0, 2, 1, 3).reshape(b, s, C_WIDTH)

    return jnp.concatenate([o_a, o_b, o_c], -1) @ w_o


def conv_ffn(h, w_gate, w_up, conv_w, w_down):
    g = causal_dwconv3(h @ w_gate, conv_w)
    return (jax.nn.silu(g) * (h @ w_up)) @ w_down


def setup_inputs(seed: int = 0) -> dict:
    key = jax.random.key(seed)
    ks = jax.random.split(key, 24)
    n = jax.random.normal
    f32 = jnp.float32
    L = DEPTH
    x = n(ks[0], (BATCH, SEQ, D_MODEL), f32)
    offset = jax.random.randint(ks[1], (BATCH, 1), 0, 64, dtype=jnp.int32) * CHUNK
    positions = (offset + jnp.arange(SEQ, dtype=jnp.int32)[None, :]).astype(jnp.int32)
    return {
        "x": x,
        "positions": positions,
        "ln_in_g": 1.0 + 0.02 * n(ks[2], (D_MODEL,), f32),
        "ln_in_b": 0.02 * n(ks[3], (D_MODEL,), f32),
        "w_in": n(ks[4], (L, D_MODEL, P_IN), f32) * D_MODEL ** -0.5,
        "diff_lambda": 0.1 * n(ks[5], (L, 4, A_DHALF), f32),
        "diff_norm_g": 1.0 + 0.02 * n(ks[6], (L, A_DV), f32),
        "conv_w": n(ks[7], (L, CONV_W, B_WIDTH), f32) * CONV_W ** -0.5,
        "mla_q_norm_g": 1.0 + 0.02 * n(ks[8], (L, C_Q_RANK), f32),
        "mla_kv_norm_g": 1.0 + 0.02 * n(ks[9], (L, C_KV_RANK), f32),
        "w_uq": n(ks[10], (L, C_Q_RANK, C_HEADS * (C_NOPE + C_ROPE)), f32) * C_Q_RANK ** -0.5,
        "w_ukv": n(ks[11], (L, C_KV_RANK, C_HEADS * (C_NOPE + C_DV)), f32) * C_KV_RANK ** -0.5,
        "w_o": n(ks[12], (L, D_MIX, D_MODEL), f32) * (D_MIX ** -0.5 * BETA),
        "ln1_g": 1.0 + 0.02 * n(ks[13], (L, D_MODEL), f32),
        "ln1_b": 0.02 * n(ks[14], (L, D_MODEL), f32),
        "ffn_w_gate": n(ks[15], (L, D_MODEL, D_FF), f32) * D_MODEL ** -0.5,
        "ffn_w_up": n(ks[16], (L, D_MODEL, D_FF), f32) * D_MODEL ** -0.5,
        "ffn_conv_w": n(ks[17], (L, CONV_W, D_FF), f32) * CONV_W ** -0.5,
        "ffn_w_down": n(ks[18], (L, D_FF, D_MODEL), f32) * (D_FF ** -0.5 * BETA),
        "ln2_g": 1.0 + 0.02 * n(ks[19], (L, D_MODEL), f32),
        "ln2_b": 0.02 * n(ks[20], (L, D_MODEL), f32),
    }


def reference(x, positions, ln_in_g, ln_in_b, w_in, diff_lambda, diff_norm_g, conv_w,
              mla_q_norm_g, mla_kv_norm_g, w_uq, w_ukv, w_o, ln1_g, ln1_b,
              ffn_w_gate, ffn_w_up, ffn_conv_w, ffn_w_down, ln2_g, ln2_b):
    h = layer_norm(x, ln_in_g, ln_in_b)
    for l in range(DEPTH):
        lam_init = 0.8 - 0.6 * math.exp(-0.3 * l)
        m = mixer(h, positions, lam_init, w_in[l], diff_lambda[l], diff_norm_g[l], conv_w[l],
                  mla_q_norm_g[l], mla_kv_norm_g[l], w_uq[l], w_ukv[l], w_o[l])
        h = layer_norm(ALPHA * h + m, ln1_g[l], ln1_b[l])
        f = conv_ffn(h, ffn_w_gate[l], ffn_w_up[l], ffn_conv_w[l], ffn_w_down[l])
        h = layer_norm(ALPHA * h + f, ln2_g[l], ln2_b[l])
    return h
```

```python
import contextlib
import math
import numpy as np
import ml_dtypes
import concourse.bass as bass
import concourse.mybir as mybir
from concourse.bass_utils import run_bass_kernel_spmd

F32, BF16, I32 = mybir.dt.float32, mybir.dt.bfloat16, mybir.dt.int32
AF = mybir.ActivationFunctionType
ALU = mybir.AluOpType
NPBF = ml_dtypes.bfloat16

NCORE = 8
S = 16384
D = 2048
TT = 512
NSLOT = 4
KC = 16
DEPTH = 2
P_IN = 4640
D_FF = 5632
FFC = 44
ALPHA = (2.0 * DEPTH) ** 0.25
LN_EPS = 1e-5
RMS_EPS = 1e-6
A_SCALE = 0.125
C_SCALE = 96.0 ** -0.5
SLOPES = [2.0 ** (-8.0 * (h + 1.0) / 6.0) for h in range(6)]
NEGBIG = -30000.0


class Ctx:
    def __init__(self):
        self.nc = bass.Bass("TRN2", target_bir_lowering=False)
        self.es = contextlib.ExitStack()
        self.sem = {}
        self.cnt = {}
        self.seen = {}
        self.n = 0

    def S(self, key):
        if key not in self.sem:
            self.sem[key] = self.es.enter_context(self.nc.semaphore("s_" + key))
            self.cnt[key] = 0
        return self.sem[key]

    def sb(self, shape, dt, name=None):
        self.n += 1
        return self.es.enter_context(self.nc.sbuf_tensor(f"sb_{name}_{self.n}", list(shape), dt))

    def ps(self, shape, dt=F32, name=None):
        self.n += 1
        return self.es.enter_context(self.nc.psum_tensor(f"ps_{name}_{self.n}", list(shape), dt))

    def dram(self, name, shape, dt, kind):
        return self.nc.dram_tensor(name, list(shape), dt, kind=kind).ap()

    def tick(self, key, inst, n=1):
        s = self.S(key)
        self.cnt[key] += n
        inst.then_inc(s, n)
        return (key, self.cnt[key])

    def wait(self, eng, *tickets):
        e = getattr(self.nc, eng)
        for t in tickets:
            if t is None:
                continue
            if isinstance(t, list):
                self.wait(eng, *t)
                continue
            key, v = t
            if self.seen.get((eng, key), 0) < v:
                e.wait_ge(self.sem[key], v)
                self.seen[(eng, key)] = v

    def op(self, eng, inst):
        return self.tick("e_" + eng, inst)

    def dma(self, eng, out, in_, key, **kw):
        inst = getattr(self.nc, eng).dma_start(out=out, in_=in_, **kw)
        return self.tick("d_" + key, inst, 16)


class Ring:
    def __init__(self, bufs):
        self.bufs = bufs
        self.free = [None] * len(bufs)
        self.i = 0

    def get(self):
        i = self.i
        self.i = (self.i + 1) % len(self.bufs)
        return i, self.bufs[i], self.free[i]

    def release(self, i, ticket):
        self.free[i] = ticket


def bf16_round(a):
    return np.asarray(a, np.float32).astype(NPBF)


def emit_ln_fm(c, pre, pre_ready, g_col, b_col, onesD, sq_ring, ps_mean, ps_e2, ps_free, stat, out32, outb,
               out_free=None):
    nc = c.nc
    mean_sb, msq, rstd, tmp_ring = stat
    c.wait("scalar", pre_ready)
    c.wait("tensor", pre_ready, ps_free)
    last_mm = None
    for k in range(KC):
        i, sq, fr = sq_ring.get()
        c.wait("scalar", fr)
        t_sq = c.op("scalar", nc.scalar.activation(out=sq[:], in_=pre[:, k, :], func=AF.Square))
        nc.tensor.matmul(ps_mean[:], lhsT=onesD[:], rhs=pre[:, k, :], start=(k == 0), stop=(k == KC - 1))
        c.wait("tensor", t_sq)
        last_mm = c.op("tensor", nc.tensor.matmul(ps_e2[:], lhsT=onesD[:], rhs=sq[:], start=(k == 0),
                                                  stop=(k == KC - 1)))
        sq_ring.release(i, last_mm)
    c.wait("scalar", last_mm)
    t1 = c.op("scalar", nc.scalar.copy(out=mean_sb[:], in_=ps_mean[:]))
    c.wait("vector", t1, last_mm)
    t2 = c.op("vector", nc.vector.tensor_tensor(out=msq[:], in0=mean_sb[:], in1=mean_sb[:], op=ALU.mult))
    c.wait("vector", t2)
    t3 = c.op("vector", nc.vector.tensor_tensor(out=msq[:], in0=ps_e2[:], in1=msq[:], op=ALU.subtract))
    c.wait("scalar", t3)
    t4 = c.op("scalar", nc.scalar.activation(out=msq[:], in_=msq[:], func=AF.Sqrt, bias=c.eps_ln[:], scale=1.0))
    c.wait("vector", t4)
    t5 = c.op("vector", nc.vector.reciprocal(out=rstd[:], in_=msq[:]))
    c.wait("vector", t5, out_free)
    c.wait("scalar", out_free)
    done = []
    for k in range(KC):
        i, tmp, fr = tmp_ring.get()
        c.wait("vector", fr)
        a = c.op("vector", nc.vector.tensor_tensor(out=tmp[:], in0=pre[:, k, :], in1=mean_sb[:], op=ALU.subtract))
        c.wait("vector", a)
        b = c.op("vector", nc.vector.tensor_tensor(out=tmp[:], in0=tmp[:], in1=rstd[:], op=ALU.mult))
        c.wait("scalar", b)
        d1 = c.op("scalar", nc.scalar.activation(out=out32[:, k, :], in_=tmp[:], func=AF.Identity,
                                                 bias=b_col[:, k:k + 1], scale=g_col[:, k:k + 1]))
        d2 = c.op("scalar", nc.scalar.activation(out=outb[:, k, :], in_=tmp[:], func=AF.Identity,
                                                 bias=b_col[:, k:k + 1], scale=g_col[:, k:k + 1]))
        tmp_ring.release(i, d2)
        done = [d1, d2]
    return done, t3


def load_cols(c, dst, src_vec, key):
    return c.dma("sync", dst, src_vec.rearrange("(k p) -> p k", p=128), key, allow_slow_non_contiguous=True)


def build_p1(layer, nslot=NSLOT):
    c = Ctx()
    nc = c.nc
    EI, EO = "ExternalInput", "ExternalOutput"
    if layer == 0:
        x_own = c.dram("x_own", [nslot, 4, 128, D], F32, EI)
        ln_g = c.dram("ln_g", [D], F32, EI)
        ln_b = c.dram("ln_b", [D], F32, EI)
        h32_out = c.dram("h32", [nslot, KC, 128, TT], F32, EO)
    else:
        hb_in = c.dram("hb", [nslot, KC, 128, TT], BF16, EI)
    pos = c.dram("pos", [nslot, TT], I32, EI)
    w_in = c.dram("w_in", [D, P_IN], F32, EI)
    w_kr2 = c.dram("w_kr2", [D, 64], F32, EI)
    w_uq = c.dram("w_uq", [512, 576], F32, EI)
    w_uq_sw = c.dram("w_uq_sw", [512, 576], F32, EI)
    w_ukv = c.dram("w_ukv", [256, 1152], F32, EI)
    gq = c.dram("gq", [512], F32, EI)
    gkv = c.dram("gkv", [256], F32, EI)
    ident_f = c.dram("ident_f", [128, 128], F32, EI)
    onesD_d = c.dram("onesD", [128, 128], F32, EI)
    ones_b_d = c.dram("ones_b", [128, 128], BF16, EI)
    ropec = c.dram("ropec", [128, 4], F32, EI)
    sig = c.dram("sig", [6, 1], F32, EI)

    qa = c.dram("qa", [nslot, 12, 70, TT], BF16, EO)
    qc = c.dram("qc", [nslot, 6, 96, TT], BF16, EO)
    ka = c.dram("ka", [nslot, 12, 70, TT], BF16, EO)
    kc = c.dram("kc", [nslot, 6, 96, TT], BF16, EO)
    va = c.dram("va", [6, 128, nslot, 4, 128], BF16, EO)
    vc = c.dram("vc", [6, 128, nslot, 4, 128], BF16, EO)
    u_o = c.dram("u", [nslot, 4, 128, TT], F32, EO)
    bb_o = c.dram("bb", [nslot, 4, 128, TT], F32, EO)
    uh_o = c.dram("uh", [nslot, 2, 512], F32, EO)

    hT = c.sb([128, KC, TT], BF16, "hT")
    wbuf = [c.sb([128, KC, 512], BF16, f"wbuf{i}") for i in range(2)]
    wring = Ring(wbuf)
    onesD = c.sb([128, 128], F32, "onesD")
    ones_b = c.sb([128, 128], BF16, "ones_b")
    identf = c.sb([128, 128], F32, "identf")
    c.eps_ln = c.sb([128, 1], F32, "eps_ln")
    eps_rms = c.sb([128, 1], F32, "eps_rms")
    ropec_sb = c.sb([128, 4], F32, "ropec")
    wuq_b = c.sb([128, 4, 576], BF16, "wuq_b")
    wuqsw_b = c.sb([128, 4, 576], BF16, "wuqsw_b")
    wukv_b = c.sb([128, 2, 1152], BF16, "wukv_b")
    wstage = c.sb([128, 4, 576], F32, "wstage")
    gq_col = c.sb([128, 4], F32, "gq_col")
    gkv_col = c.sb([128, 2], F32, "gkv_col")
    sig_col = c.sb([6, 1], F32, "sig_col")
    cqT = c.sb([128, 4, TT], BF16, "cqT")
    ckvT = c.sb([128, 2, TT], BF16, "ckvT")
    sqb = [c.sb([128, TT], BF16, f"sqb{i}") for i in range(6)]
    sqkvT = sqb[4:6]
    rq = c.sb([128, TT], F32, "rq")
    rkv = c.sb([128, TT], F32, "rkv")
    rkv_col = c.sb([128, 4], F32, "rkv_col")
    cosT = c.sb([128, TT], F32, "cosT")
    sinT = c.sb([128, TT], F32, "sinT")
    posi = c.sb([128, TT], I32, "posi")
    posf = c.sb([128, TT], F32, "posf")
    ra = c.sb([128, TT], F32, "ra")
    rb = c.sb([128, TT], F32, "rb")
    rc = c.sb([128, TT], F32, "rc")
    ri = c.sb([128, TT], I32, "ri")
    bh_sb = c.sb([128, 4, TT], F32, "bh_sb")
    ev32 = Ring([c.sb([128, TT], F32, f"ev32_{i}") for i in range(2)])
    evb = Ring([c.sb([128, TT], BF16, f"evb_{i}") for i in range(4)])
    evv = Ring([c.sb([128, 768], BF16, f"evv_{i}") for i in range(2)])
    tq = c.sb([128, TT], F32, "tq")
    tq2 = c.sb([128, TT], F32, "tq2")
    apc = [c.sb([6, TT], BF16, f"apc{i}") for i in range(6)]
    apr = [c.sb([6, TT], F32, f"apr{i}") for i in range(3)]
    onesrow = c.sb([12, 3, TT], BF16, "onesrow")
    if layer == 0:
        xt = [c.sb([128, D], F32, f"xt{i}") for i in range(2)]
        pre = c.sb([128, KC, TT], F32, "pre")
        h32 = pre
        lng = c.sb([128, KC], F32, "lng")
        lnb = c.sb([128, KC], F32, "lnb")
        ln_sq = Ring([c.sb([128, TT], F32, f"lnsq{i}") for i in range(2)])
        ln_stat = (c.sb([128, TT], F32, "ln_mean"), c.sb([128, TT], F32, "ln_msq"), c.sb([128, TT], F32, "ln_rstd"),
                   Ring([c.sb([128, TT], F32, f"lntmp{i}") for i in range(2)]))

    banks = [c.ps([128, TT], F32, f"bank{i}") for i in range(8)]
    pring = Ring(banks)

    t_c = [c.dma("sync", onesD[:], onesD_d, "const"), c.dma("sync", ones_b[:], ones_b_d, "const"),
           c.dma("sync", identf[:], ident_f, "const"), c.dma("sync", ropec_sb[:], ropec, "const"),
           c.dma("sync", sig_col[:], sig, "const"),
           load_cols(c, gq_col[:], gq, "const"), load_cols(c, gkv_col[:], gkv, "const")]
    if layer == 0:
        t_c += [load_cols(c, lng[:], ln_g, "const"), load_cols(c, lnb[:], ln_b, "const")]
    t_c = [t_c[-1]]
    nc.vector.memset(c.eps_ln[:], LN_EPS)
    nc.vector.memset(eps_rms[:], RMS_EPS)
    t_ms = c.op("vector", nc.vector.memset(onesrow[:], 1.0))
    for e in ("vector", "scalar", "tensor", "gpsimd"):
        c.wait(e, t_c, t_ms)
    for (wd, dst) in ((w_uq, wuq_b), (w_uq_sw, wuqsw_b)):
        t = c.dma("sync", wstage[:], wd.rearrange("(k p) n -> p k n", p=128), "wst")
        c.wait("vector", t)
        tv = None
        for k in range(4):
            tv = c.op("vector", nc.vector.tensor_scalar(out=dst[:, k, :], in0=wstage[:, k, :],
                                                       scalar1=gq_col[:, k:k + 1], scalar2=None, op0=ALU.mult))
        c.wait("sync", tv)
    wst2 = wstage[:].rearrange("p k n -> p (k n)")[:, 0:2304].rearrange("p (k n) -> p k n", k=2)
    t = c.dma("sync", wst2, w_ukv.rearrange("(k p) n -> p k n", p=128), "wst")
    c.wait("vector", t)
    tv = None
    for k in range(2):
        tv = c.op("vector", nc.vector.tensor_scalar(out=wukv_b[:, k, :], in0=wst2[:, k, :],
                                                   scalar1=gkv_col[:, k:k + 1], scalar2=None, op0=ALU.mult))
    t_smallw = tv
    c.wait("tensor", t_smallw)

    w_fm = w_in.rearrange("(k p) n -> p k n", p=128)
    state = {"hT_free": None}
    out_dmas = []

    def fm_group(src_ap, ncols, consume, nk=KC, xT=None, kcn=KC):
        wi, wt, wfree = wring.get()
        c.wait("gpsimd", wfree)
        tw = c.dma("gpsimd", wt[:, 0:kcn, 0:ncols], src_ap, f"w{wi}")
        c.wait("tensor", tw)
        lastmm = None
        nj = (ncols + 127) // 128
        for j in range(nj):
            cw = min(128, ncols - j * 128)
            bi, bank, bfree = pring.get()
            c.wait("tensor", bfree)
            for k in range(kcn):
                mm = nc.tensor.matmul(bank[0:cw, :], lhsT=wt[:, k, j * 128:j * 128 + cw], rhs=xT[:, k, :],
                                      start=(k == 0), stop=(k == kcn - 1))
            lastmm = c.op("tensor", mm)
            rel = consume(j, bank, lastmm)
            pring.release(bi, rel)
        wring.release(wi, lastmm)

    for m in range(nslot):
        if layer == 0:
            c.wait("sync", state["hT_free"])
            pre_done = None
            for tt in range(4):
                xb = xt[tt % 2]
                c.wait("sync", state.get(f"xt_free{tt % 2}"))
                tx = c.dma("sync", xb[:], x_own[m, tt], f"x{tt % 2}")
                c.wait("tensor", tx)
                for q in range(4):
                    bi, bank, bfree = pring.get()
                    c.wait("tensor", bfree)
                    for j in range(4):
                        k = 4 * q + j
                        mm = nc.tensor.transpose(bank[:, j * 128:(j + 1) * 128], xb[:, k * 128:(k + 1) * 128],
                                                 identf[:])
                    tmm = c.op("tensor", mm)
                    c.wait("scalar", tmm, state["hT_free"])
                    pre_done = c.op("scalar", nc.scalar.copy(
                        out=pre[:, 4 * q:4 * q + 4, tt * 128:(tt + 1) * 128],
                        in_=bank[:].rearrange("p (j t) -> p j t", j=4)))
                    pring.release(bi, pre_done)
                state[f"xt_free{tt % 2}"] = tmm
            bi1, pm, f1 = pring.get()
            bi2, pe2, f2 = pring.get()
            done, psfree = emit_ln_fm(c, pre, pre_done, lng, lnb, onesD, ln_sq, pm, pe2, [f1, f2], ln_stat, h32, hT,
                                      out_free=state["hT_free"])
            pring.release(bi1, psfree)
            pring.release(bi2, psfree)
            c.wait("sync", done)
            th = c.dma("sync", h32_out[m].rearrange("k p t -> p k t"), h32[:], "h32o")
            out_dmas.append(th)
            c.wait("scalar", th)
            hT_ready = done
        else:
            c.wait("sync", state["hT_free"])
            hT_ready = [c.dma("sync", hT[:], hb_in[m].rearrange("k p t -> p k t"), "hT")]
        c.wait("tensor", hT_ready)

        c.wait("sync", state.get("pos_free"))
        tp = c.dma("sync", posi[:], pos[m].partition_broadcast(128), "pos")
        c.wait("vector", tp)
        V = nc.vector

        def vop(inst):
            t = c.op("vector", inst)
            c.wait("vector", t)
            return t
        vop(V.tensor_copy(out=posf[:], in_=posi[:]))
        vop(V.tensor_scalar(out=ri[:], in0=posi[:], scalar1=7, scalar2=None, op0=ALU.arith_shift_right))
        vop(V.tensor_copy(out=ra[:], in_=ri[:]))
        vop(V.tensor_scalar(out=ri[:], in0=posi[:], scalar1=127, scalar2=None, op0=ALU.bitwise_and))
        vop(V.tensor_copy(out=rb[:], in_=ri[:]))
        vop(V.tensor_scalar(out=ra[:], in0=ra[:], scalar1=ropec_sb[:, 0:1], scalar2=None, op0=ALU.mult))
        vop(V.scalar_tensor_tensor(out=ra[:], in0=rb[:], scalar=ropec_sb[:, 1:2], in1=ra[:], op0=ALU.mult,
                                   op1=ALU.add))
        vop(V.tensor_copy(out=ri[:], in_=ra[:]))
        vop(V.tensor_copy(out=rb[:], in_=ri[:]))
        vop(V.tensor_tensor(out=ra[:], in0=ra[:], in1=rb[:], op=ALU.subtract))

        def wrap(dst):
            vop(V.tensor_scalar(out=rb[:], in0=dst[:], scalar1=0.5, scalar2=None, op0=ALU.is_gt))
            vop(V.tensor_tensor(out=dst[:], in0=dst[:], in1=rb[:], op=ALU.subtract))
            vop(V.tensor_scalar(out=rb[:], in0=dst[:], scalar1=-0.5, scalar2=None, op0=ALU.is_lt))
            return vop(V.tensor_tensor(out=dst[:], in0=dst[:], in1=rb[:], op=ALU.add))
        wrap(ra)
        vop(V.tensor_scalar(out=rc[:], in0=ra[:], scalar1=0.25, scalar2=None, op0=ALU.add))
        tw2 = wrap(rc)
        c.wait("scalar", tw2, state.get("tab_free"))
        TWO_PI = 6.283185
        ts1 = c.op("scalar", nc.scalar.activation(out=sinT[:], in_=ra[:], func=AF.Sin, scale=TWO_PI))
        ts2 = c.op("scalar", nc.scalar.activation(out=cosT[:], in_=rc[:], func=AF.Sin, scale=TWO_PI))
        c.wait("vector", ts1, ts2)
        t_tab = vop(V.tensor_scalar(out=sinT[:], in0=sinT[:], scalar1=ropec_sb[:, 2:3], scalar2=None, op0=ALU.mult))
        c.wait("vector", state.get("apc_free"))
        vop(V.tensor_scalar(out=apr[0][:], in0=posf[0:6, :], scalar1=sig_col[:, 0:1], scalar2=None, op0=ALU.mult))
        vop(V.tensor_copy(out=apc[0][:], in_=apr[0][:]))
        vop(V.tensor_tensor(out=apr[1][:], in0=apr[0][:], in1=apc[0][:], op=ALU.subtract))
        vop(V.tensor_copy(out=apc[1][:], in_=apr[1][:]))
        vop(V.tensor_tensor(out=apr[2][:], in0=apr[1][:], in1=apc[1][:], op=ALU.subtract))
        vop(V.tensor_copy(out=apc[2][:], in_=apr[2][:]))
        tl = None
        for i in range(3):
            tl = vop(V.tensor_scalar(out=apc[3 + i][:], in0=apc[i][:], scalar1=-1.0, scalar2=None, op0=ALU.mult))
        state["pos_free"] = tl
        c.wait("sync", tl)
        ta = []
        for i in range(3):
            for j in range(2):
                ta.append(c.dma("sync", ka[m, j::2, 67 + i, :], apc[i][:], "aug"))
                ta.append(c.dma("sync", qa[m, j::2, 64 + i, :], apc[3 + i][:], "aug"))
        ta.append(c.dma("sync", ka[m, :, 64:67, :], onesrow[:], "aug"))
        ta.append(c.dma("sync", qa[m, :, 67:70, :], onesrow[:], "aug"))
        state["apc_free"] = ta[-1]
        out_dmas.append(ta[-1])

        def mk_qk(dst, base):
            def consume(j, bank, mm):
                h = base + j
                i, ev, fr = evb.get()
                c.wait("scalar", mm, fr)
                t = c.op("scalar", nc.scalar.copy(out=ev[:], in_=bank[:]))
                c.wait("sync", t)
                c.dma("sync", dst[m, 2 * h, 0:64, :], ev[0:64, :], f"evb{i}")
                d = c.dma("sync", dst[m, 2 * h + 1, 0:64, :], ev[64:128, :], f"evb{i}")
                evb.release(i, d)
                return t
            return consume
        fm_group(w_fm[:, :, 0:512], 512, mk_qk(qa, 0), xT=hT)
        fm_group(w_fm[:, :, 512:768], 256, mk_qk(qa, 4), xT=hT)
        fm_group(w_fm[:, :, 768:1280], 512, mk_qk(ka, 0), xT=hT)
        fm_group(w_fm[:, :, 1280:1536], 256, mk_qk(ka, 4), xT=hT)

        def cons_bh(j, bank, mm):
            c.wait("scalar", mm, state.get("bh_free"))
            return c.op("scalar", nc.scalar.copy(out=bh_sb[:, j, :], in_=bank[:]))

        def cons_bc(j, bank, mm):
            i, ev, fr = ev32.get()
            c.wait("vector", mm, fr, state.get("bh_t"))
            t = c.op("vector", nc.vector.tensor_tensor(out=ev[:], in0=bank[:], in1=bh_sb[:, j, :], op=ALU.mult))
            c.wait("sync", t)
            c.dma("sync", u_o[m, j], ev[:], f"ev32{i}")
            d = c.dma("sync", uh_o[m, :, j * 128:(j + 1) * 128].rearrange("t p -> p t"), ev[:, TT - 2:TT],
                      f"ev32{i}", allow_slow_non_contiguous=True)
            ev32.release(i, d)
            state["bh_free"] = t
            return t

        def cons_bb(j, bank, mm):
            i, ev, fr = ev32.get()
            c.wait("scalar", mm, fr)
            t = c.op("scalar", nc.scalar.copy(out=ev[:], in_=bank[:]))
            c.wait("sync", t)
            d = c.dma("sync", bb_o[m, j], ev[:], f"ev32{i}")
            ev32.release(i, d)
            return t
        fm_group(w_fm[:, :, 2304:2816], 512, cons_bh, xT=hT)
        state["bh_t"] = ("e_scalar", c.cnt["e_scalar"])
        fm_group(w_fm[:, :, 3328:3840], 512, cons_bc, xT=hT)
        fm_group(w_fm[:, :, 2816:3328], 512, cons_bb, xT=hT)

        sq_t = {}

        def mk_c(dstT, sqs, tag):
            def consume(j, bank, mm):
                c.wait("scalar", mm, state.get("cq_free"))
                t1 = c.op("scalar", nc.scalar.copy(out=dstT[:, j, :], in_=bank[:]))
                t2 = c.op("scalar", nc.scalar.activation(out=sqs[j][:], in_=bank[:], func=AF.Square))
                sq_t[(tag, j)] = t2
                return t2
            return consume
        fm_group(w_fm[:, :, 3840:4352], 512, mk_c(cqT, sqb[0:4], "q"), xT=hT)
        fm_group(w_fm[:, :, 4352:4608], 256, mk_c(ckvT, sqkvT, "kv"), xT=hT)

        def rms_r(sqs, n, dst, nfeat):
            bi, bank, bfree = pring.get()
            c.wait("tensor", bfree, [sq_t[k] for k in sq_t])
            for j in range(n):
                mm = nc.tensor.matmul(bank[:], lhsT=ones_b[:], rhs=sqs[j][:], start=(j == 0), stop=(j == n - 1))
            tmm = c.op("tensor", mm)
            c.wait("scalar", tmm, state.get("r_free"))
            t = c.op("scalar", nc.scalar.activation(out=dst[:], in_=bank[:], func=AF.Sqrt, bias=eps_rms[:],
                                                    scale=1.0 / nfeat))
            pring.release(bi, t)
            c.wait("vector", t)
            return vop(nc.vector.reciprocal(out=dst[:], in_=dst[:]))
        t_rq = rms_r(sqb[0:4], 4, rq, 512.0)
        t_rkv = rms_r(sqkvT, 2, rkv, 256.0)
        bi, bank, bfree = pring.get()
        c.wait("tensor", bfree)
        for tt in range(4):
            for j in range(2):
                mm = nc.tensor.matmul(bank[:, tt:tt + 1], lhsT=sqkvT[j][:, tt * 128:(tt + 1) * 128], rhs=ones_b[:, 0:1],
                                      start=(j == 0), stop=(j == 1))
        tmm = c.op("tensor", mm)
        c.wait("scalar", tmm)
        t = c.op("scalar", nc.scalar.activation(out=rkv_col[:], in_=bank[:, 0:4], func=AF.Sqrt, bias=eps_rms[:],
                                                scale=1.0 / 256.0))
        pring.release(bi, t)
        c.wait("vector", t)
        t_rkvc = vop(nc.vector.reciprocal(out=rkv_col[:], in_=rkv_col[:]))
        state["cq_free"] = None

        wi, wt, wfree = wring.get()
        c.wait("gpsimd", wfree)
        tw = c.dma("gpsimd", wt[:, :, 0:64], w_kr2.rearrange("(k p) n -> p k n", p=128), f"w{wi}")
        c.wait("tensor", tw)
        b1, bank1, f1 = pring.get()
        b2, bank2, f2 = pring.get()
        c.wait("tensor", f1, f2)
        for k in range(KC):
            nc.tensor.matmul(bank1[0:32, :], lhsT=wt[:, k, 0:32], rhs=hT[:, k, :], start=(k == 0), stop=(k == KC - 1))
        for k in range(KC):
            mm = nc.tensor.matmul(bank2[0:32, :], lhsT=wt[:, k, 32:64], rhs=hT[:, k, :], start=(k == 0),
                                  stop=(k == KC - 1))
        tmm = c.op("tensor", mm)
        wring.release(wi, tmm)
        c.wait("vector", tmm, t_tab)
        vop(V.tensor_tensor(out=tq[0:32, :], in0=bank1[0:32, :], in1=cosT[0:32, :], op=ALU.mult))
        t2 = vop(V.tensor_tensor(out=tq2[0:32, :], in0=bank2[0:32, :], in1=sinT[0:32, :], op=ALU.mult))
        pring.release(b1, t2)
        pring.release(b2, t2)
        i, ev, fr = evb.get()
        c.wait("vector", fr)
        t3 = vop(V.tensor_tensor(out=ev[0:32, :], in0=tq[0:32, :], in1=tq2[0:32, :], op=ALU.add))
        c.wait("sync", t3)
        for h in range(6):
            d = c.dma("sync", kc[m, h, 64:96, :], ev[0:32, :], f"evb{i}")
        evb.release(i, d)

        for h in range(6):
            b1, bank1, f1 = pring.get()
            b2, bank2, f2 = pring.get()
            c.wait("tensor", f1, f2)
            for k in range(4):
                nc.tensor.matmul(bank1[0:96, :], lhsT=wuq_b[:, k, h * 96:(h + 1) * 96], rhs=cqT[:, k, :],
                                 start=(k == 0), stop=(k == 3))
            for k in range(4):
                mm = nc.tensor.matmul(bank2[0:96, :], lhsT=wuqsw_b[:, k, h * 96:(h + 1) * 96], rhs=cqT[:, k, :],
                                      start=(k == 0), stop=(k == 3))
            tmm = c.op("tensor", mm)
            c.wait("vector", tmm, t_rq, t_tab)
            i, ev, fr = evb.get()
            c.wait("vector", fr)
            vop(V.tensor_tensor(out=tq[0:96, :], in0=bank1[0:96, :], in1=rq[0:96, :], op=ALU.mult))
            vop(V.tensor_tensor(out=tq2[64:96, :], in0=bank2[64:96, :], in1=rq[64:96, :], op=ALU.mult))
            tf = vop(V.tensor_copy(out=ev[0:64, :], in_=tq[0:64, :]))
            pring.release(b1, tf)
            pring.release(b2, tf)
            vop(V.tensor_tensor(out=tq[64:96, :], in0=tq[64:96, :], in1=cosT[64:96, :], op=ALU.mult))
            vop(V.tensor_tensor(out=tq2[64:96, :], in0=tq2[64:96, :], in1=sinT[64:96, :], op=ALU.mult))
            t3 = vop(V.tensor_tensor(out=ev[64:96, :], in0=tq[64:96, :], in1=tq2[64:96, :], op=ALU.add))
            c.wait("sync", t3)
            d = c.dma("sync", qc[m, h], ev[0:96, :], f"evb{i}")
            evb.release(i, d)
        state["tab_free"] = ("e_vector", c.cnt["e_vector"])

        for j in range(3):
            bi, bank, bfree = pring.get()
            c.wait("tensor", bfree)
            for k in range(2):
                mm = nc.tensor.matmul(bank[:], lhsT=wukv_b[:, k, j * 128:(j + 1) * 128], rhs=ckvT[:, k, :],
                                      start=(k == 0), stop=(k == 1))
            tmm = c.op("tensor", mm)
            i, ev, fr = evb.get()
            c.wait("vector", tmm, fr, t_rkv)
            t = vop(V.tensor_tensor(out=ev[:], in0=bank[:], in1=rkv[:], op=ALU.mult))
            pring.release(bi, t)
            c.wait("sync", t)
            c.dma("sync", kc[m, 2 * j, 0:64, :], ev[0:64, :], f"evb{i}")
            d = c.dma("sync", kc[m, 2 * j + 1, 0:64, :], ev[64:128, :], f"evb{i}")
            evb.release(i, d)

        for tt in range(4):
            b1, bank1, f1 = pring.get()
            b2, bank2, f2 = pring.get()
            c.wait("tensor", f1, f2)
            for k in range(2):
                nc.tensor.matmul(bank1[:], lhsT=ckvT[:, k, tt * 128:(tt + 1) * 128], rhs=wukv_b[:, k, 384:896],
                                 start=(k == 0), stop=(k == 1))
            for k in range(2):
                mm = nc.tensor.matmul(bank2[:, 0:256], lhsT=ckvT[:, k, tt * 128:(tt + 1) * 128],
                                      rhs=wukv_b[:, k, 896:1152], start=(k == 0), stop=(k == 1))
            tmm = c.op("tensor", mm)
            i, ev, fr = evv.get()
            c.wait("scalar", tmm, fr, t_rkvc)
            c.op("scalar", nc.scalar.activation(out=ev[:, 0:512], in_=bank1[:], func=AF.Copy,
                                                scale=rkv_col[:, tt:tt + 1]))
            t = c.op("scalar", nc.scalar.activation(out=ev[:, 512:768], in_=bank2[:, 0:256], func=AF.Copy,
                                                    scale=rkv_col[:, tt:tt + 1]))
            pring.release(b1, t)
            pring.release(b2, t)
            c.wait("sync", t)
            d = c.dma("sync", vc[:, :, m, tt, :].rearrange("h p d -> p h d"),
                      ev[:].rearrange("p (h d) -> p h d", h=6), f"evv{i}")
            evv.release(i, d)
        state["r_free"] = ("e_vector", c.cnt["e_vector"])

        wi, wt, wfree = wring.get()
        c.wait("gpsimd", wfree)
        tw = c.dma("gpsimd", wt[:, :, 0:512], w_fm[:, :, 1536:2048], f"w{wi}")
        wi2, wt2, wfree2 = wring.get()
        c.wait("gpsimd", wfree2)
        tw2_ = c.dma("gpsimd", wt2[:, :, 0:256], w_fm[:, :, 2048:2304], f"w{wi2}")
        c.wait("tensor", tw, tw2_)
        for tt in range(4):
            b1, bank1, f1 = pring.get()
            b2, bank2, f2 = pring.get()
            c.wait("tensor", f1, f2)
            for k in range(KC):
                nc.tensor.matmul(bank1[:], lhsT=hT[:, k, tt * 128:(tt + 1) * 128], rhs=wt[:, k, 0:512],
                                 start=(k == 0), stop=(k == KC - 1))
            for k in range(KC):
                mm = nc.tensor.matmul(bank2[:, 0:256], lhsT=hT[:, k, tt * 128:(tt + 1) * 128], rhs=wt2[:, k, 0:256],
                                      start=(k == 0), stop=(k == KC - 1))
            tmm = c.op("tensor", mm)
            i, ev, fr = evv.get()
            c.wait("scalar", tmm, fr)
            c.op("scalar", nc.scalar.copy(out=ev[:, 0:512], in_=bank1[:]))
            t = c.op("scalar", nc.scalar.copy(out=ev[:, 512:768], in_=bank2[:, 0:256]))
            pring.release(b1, t)
            pring.release(b2, t)
            c.wait("sync", t)
            d = c.dma("sync", va[:, :, m, tt, :].rearrange("h p d -> p h d"),
                      ev[:].rearrange("p (h d) -> p h d", h=6), f"evv{i}")
            evv.release(i, d)
        wring.release(wi, tmm)
        wring.release(wi2, tmm)
        state["hT_free"] = tmm

    for key in list(c.sem.keys()):
        if key.startswith("d_"):
            c.wait("sync", (key, c.cnt[key]))
    c.es.close()
    return nc


def build_p2a(layer, nslot=NSLOT, slot_list=None):
    c = Ctx()
    nc = c.nc
    V = nc.vector
    EI, EO = "ExternalInput", "ExternalOutput"
    lam_init = 0.8 - 0.6 * math.exp(-0.3 * layer)
    slots = list(range(nslot)) if slot_list is None else slot_list
    qa = c.dram("qa", [NSLOT, 12, 70, TT], BF16, EI)
    qc = c.dram("qc", [NSLOT, 6, 96, TT], BF16, EI)
    ka = c.dram("ka", [NSLOT, 12, 70, TT], BF16, EI)
    kc = c.dram("kc", [NSLOT, 6, 96, TT], BF16, EI)
    va = c.dram("va", [6, 128, NSLOT, 4, 128], BF16, EI)
    vc = c.dram("vc", [6, 128, NSLOT, 4, 128], BF16, EI)
    ka_all = c.dram("ka_all", [NCORE, NSLOT, 12, 70, TT], BF16, EI)
    kc_all = c.dram("kc_all", [NCORE, NSLOT, 6, 96, TT], BF16, EI)
    va_all = c.dram("va_all", [NCORE, 6, 128, NSLOT, 4, 128], BF16, EI)
    vc_all = c.dram("vc_all", [NCORE, 6, 128, NSLOT, 4, 128], BF16, EI)
    u_i = c.dram("u", [NSLOT, 4, 128, TT], F32, EI)
    bb_i = c.dram("bb", [NSLOT, 4, 128, TT], F32, EI)
    uh_all = c.dram("uh_all", [64, 512], F32, EI)
    sel = c.dram("sel", [64, 8], F32, EI)
    h32_i = c.dram("h32", [NSLOT, KC, 128, TT], F32, EI)
    w_o = c.dram("w_o", [D, D], F32, EI)
    conv_w = c.dram("conv_w", [3, 512], F32, EI)
    dlam = c.dram("dlam", [4, 64], F32, EI)
    gA = c.dram("gA", [128], F32, EI)
    ln_g = c.dram("ln_g", [D], F32, EI)
    ln_b = c.dram("ln_b", [D], F32, EI)
    onesD_d = c.dram("onesD", [128, 128], F32, EI)
    ones128_d = c.dram("ones128", [128, 128], F32, EI)
    ones_b_d = c.dram("ones_b", [128, 128], BF16, EI)
    ident_b_d = c.dram("ident_b", [128, 128], BF16, EI)
    sigI_d = c.dram("sigI", [6, 128, 128], BF16, EI)
    cma_d = c.dram("cma", [4, 128, TT], BF16, EI)
    mkc_d = c.dram("mkc", [4, 128, TT], BF16, EI)
    vis_d = c.dram("vis", [128, 8], F32, EI)
    h1_32 = c.dram("h1_32", [NSLOT, KC, 128, TT], F32, EO)
    h1b = c.dram("h1b", [NSLOT, KC, 128, TT], BF16, EO)
    h1h = c.dram("h1h", [NSLOT, 2, D], BF16, EO)

    QA = c.sb([70, 12, TT], BF16, "QA")
    QC = c.sb([96, 6, TT], BF16, "QC")
    kring = Ring([c.sb([96, 2, TT], BF16, f"kt{i}") for i in range(4)])
    vring = Ring([c.sb([128, 4, 128], BF16, f"vt{i}") for i in range(4)])
    pbufs = Ring([c.sb([128, TT], BF16, f"pb{i}") for i in range(4)])
    omix = c.sb([128, KC, TT], BF16, "omix")
    pre = c.sb([128, KC, TT], F32, "pre")
    h1b_sb = c.sb([128, KC, TT], BF16, "h1b_sb")
    wring = Ring([c.sb([128, KC, 512], BF16, f"wbuf{i}") for i in range(2)])
    onesD = c.sb([128, 128], F32, "onesD")
    ones128 = c.sb([128, 128], F32, "ones128")
    ones_b = c.sb([128, 128], BF16, "ones_b")
    ident_b = c.sb([128, 128], BF16, "ident_b")
    sigI = c.sb([128, 6, 128], BF16, "sigI")
    cma = c.sb([128, 4, TT], BF16, "cma")
    mkc = c.sb([128, 4, TT], BF16, "mkc")
    vis = c.sb([128, 8], F32, "vis")
    c.eps_ln = c.sb([128, 1], F32, "eps_ln")
    eps_rms = c.sb([128, 1], F32, "eps_rms")
    lng = c.sb([128, KC], F32, "lng")
    lnb = c.sb([128, KC], F32, "lnb")
    gA_col = c.sb([128, 1], F32, "gA_col")
    dl = c.sb([128, 4, 64], F32, "dl")
    dtmp = c.sb([128, 64], F32, "dtmp")
    lam2 = c.sb([128, 4], F32, "lam2")
    neglam = c.sb([128, 1], F32, "neglam")
    cw = c.sb([128, 3, 4], F32, "cw")
    uh_sb = c.sb([64, 512], F32, "uh_sb")
    sel_sb = c.sb([64, 8], F32, "sel_sb")
    uhalo = c.sb([128, 4, 8], F32, "uhalo")
    ubuf = c.sb([128, TT + 2], F32, "ubuf")
    bbuf = c.sb([128, TT], F32, "bbuf")
    e1 = c.sb([128, TT], F32, "e1")
    e2 = c.sb([128, TT], F32, "e2")
    e3 = c.sb([128, TT], F32, "e3")
    e4 = c.sb([128, TT], F32, "e4")
    ln_sq = Ring([c.sb([128, TT], F32, f"lnsq{i}") for i in range(2)])
    ln_stat = (c.sb([128, TT], F32, "ln_mean"), c.sb([128, TT], F32, "ln_msq"), c.sb([128, TT], F32, "ln_rstd"),
               Ring([e3, e4]))
    banks = [c.ps([128, TT], F32, f"bank{i}") for i in range(8)]
    sring = Ring(banks[0:3])
    accb = banks[3:7]
    misc = banks[7]

    def vop(inst):
        t = c.op("vector", inst)
        c.wait("vector", t)
        return t

    def aop(inst):
        t = c.op("scalar", inst)
        c.wait("scalar", t)
        return t

    tcs = [c.dma("sync", onesD[:], onesD_d, "const"), c.dma("sync", ones128[:], ones128_d, "const"),
           c.dma("sync", ones_b[:], ones_b_d, "const"), c.dma("sync", ident_b[:], ident_b_d, "const"),
           c.dma("sync", sigI[:], sigI_d.rearrange("h p n -> p h n"), "const"),
           c.dma("sync", cma[:], cma_d.rearrange("k p n -> p k n"), "const"),
           c.dma("sync", mkc[:], mkc_d.rearrange("k p n -> p k n"), "const"),
           c.dma("sync", vis[:], vis_d, "const"),
           load_cols(c, lng[:], ln_g, "const"), load_cols(c, lnb[:], ln_b, "const"),
           load_cols(c, gA_col[:], gA, "const"),
           c.dma("sync", dl[:], dlam.partition_broadcast(128), "const"),
           load_cols(c, cw[:, 0, :], conv_w[0], "const"), load_cols(c, cw[:, 1, :], conv_w[1], "const"),
           load_cols(c, cw[:, 2, :], conv_w[2], "const"),
           c.dma("sync", uh_sb[:], uh_all, "const"), c.dma("sync", sel_sb[:], sel, "const")]
    tcs = [tcs[-1]]
    V.memset(c.eps_ln[:], LN_EPS)
    t_ms = c.op("vector", V.memset(eps_rms[:], RMS_EPS))
    for e in ("vector", "scalar", "tensor", "gpsimd"):
        c.wait(e, tcs, t_ms)
    for q in range(2):
        vop(V.tensor_tensor(out=dtmp[:], in0=dl[:, 2 * q, :], in1=dl[:, 2 * q + 1, :], op=ALU.mult))
        c.wait("scalar", ("e_vector", c.cnt["e_vector"]))
        aop(nc.scalar.activation(out=dtmp[:], in_=dtmp[:], func=AF.Copy, accum_out=lam2[:, q:q + 1]))
        c.wait("vector", ("e_scalar", c.cnt["e_scalar"]))
    aop(nc.scalar.activation(out=lam2[:, 2:4], in_=lam2[:, 0:2], func=AF.Exp))
    c.wait("vector", ("e_scalar", c.cnt["e_scalar"]))
    vop(V.tensor_tensor(out=neglam[:], in0=lam2[:, 3:4], in1=lam2[:, 2:3], op=ALU.subtract))
    vop(V.tensor_scalar(out=neglam[:], in0=neglam[:], scalar1=-lam_init, scalar2=None, op0=ALU.add))
    t_setup = vop(V.tensor_scalar(out=gA_col[:], in0=gA_col[:], scalar1=(1.0 - lam_init), scalar2=None, op0=ALU.mult))
    for j in range(4):
        mm = nc.tensor.matmul(misc[:, j * 8:(j + 1) * 8], lhsT=uh_sb[0:64, j * 128:(j + 1) * 128], rhs=sel_sb[0:64, :],
                              start=True, stop=True)
    tmm = c.op("tensor", mm)
    c.wait("vector", tmm)
    misc_free = vop(V.tensor_copy(out=uhalo[:].rearrange("p j e -> p (j e)"), in_=misc[:, 0:32]))
    c.wait("scalar", t_setup)

    w_fm = w_o.rearrange("(k p) n -> p k n", p=128)
    st = {"acc_free": None, "misc_free": misc_free, "omix_free": None, "pre_free": None, "h1b_free": None,
          "q_free": None}
    LOOK = 2

    for m in slots:
        ngroups = 8 * m + 8
        c.wait("sync", st["q_free"])
        tq_ = [c.dma("sync", QA[:], qa[m].rearrange("h r t -> r h t"), "q"),
               c.dma("sync", QC[:], qc[m].rearrange("h r t -> r h t"), "q")]
        c.wait("tensor", tq_[-1])
        for kind, h in [("A", hh) for hh in range(6)] + [("C", hh) for hh in range(6)]:
            nmap = 2 if kind == "A" else 1
            nrow = 70 if kind == "A" else 96
            loads = {}

            def issue_load(G):
                if G >= ngroups or G in loads:
                    return
                ki, ktb, kfree = kring.get()
                vi, vtb, vfree = vring.get()
                c.wait("sync", kfree, vfree)
                own = (G == ngroups - 1)
                ip, mp = G % 8, G // 8
                if kind == "A":
                    ks = ka[m, 2 * h:2 * h + 2] if own else ka_all[ip, mp, 2 * h:2 * h + 2]
                    tk = c.dma("sync", ktb[0:70, :, :], ks.rearrange("j r t -> r j t"), f"k{ki}")
                    vs = va[h, :, m] if own else va_all[ip, h, :, mp]
                else:
                    ks = kc[m, h] if own else kc_all[ip, mp, h]
                    tk = c.dma("sync", ktb[0:96, 0, :], ks, f"k{ki}")
                    vs = vc[h, :, m] if own else vc_all[ip, h, :, mp]
                tv = c.dma("sync", vtb[:], vs, f"v{vi}")
                loads[G] = (ki, ktb, tk, vi, vtb, tv)
            issue_load(0)
            issue_load(1)
            units = [(G, kt, j) for G in range(ngroups) for kt in range(4) for j in range(nmap)]
            nun = len(units)
            pend = []
            c.wait("tensor", st["acc_free"])
            last_pe = [None]

            def emit_pv(item):
                idx, G, kt, j, pi, pbuf, t_p = item
                ki, ktb, tk, vi, vtb, tv = loads[G]
                c.wait("tensor", t_p, tv)
                first = (idx < nmap)
                last = (idx >= nun - nmap)
                nc.tensor.matmul(accb[j][:], lhsT=vtb[:, kt, :], rhs=pbuf[:], start=first, stop=last)
                mm = nc.tensor.matmul(accb[2 + j][:], lhsT=ones_b[:], rhs=pbuf[:], start=first, stop=last)
                t = c.op("tensor", mm)
                pbufs.release(pi, t)
                if kt == 3 and j == nmap - 1:
                    kring.release(ki, t)
                    vring.release(vi, t)
                last_pe[0] = t

            for idx, (G, kt, j) in enumerate(units):
                if kt == 0 and j == 0:
                    issue_load(G + 2)
                ki, ktb, tk, vi, vtb, tv = loads[G]
                own = (G == ngroups - 1)
                si, sbank, sfree = sring.get()
                c.wait("tensor", sfree, tk)
                if kind == "A":
                    mm = nc.tensor.matmul(sbank[:], lhsT=ktb[0:70, j, kt * 128:(kt + 1) * 128], rhs=QA[0:70, 2 * h + j, :],
                                          start=True, stop=(not own))
                    if own:
                        mm = nc.tensor.matmul(sbank[:], lhsT=sigI[:, h, :], rhs=cma[:, kt, :], start=False, stop=True)
                else:
                    mm = nc.tensor.matmul(sbank[:], lhsT=ktb[0:96, 0, kt * 128:(kt + 1) * 128], rhs=QC[0:96, h, :],
                                          start=True, stop=(not own))
                    if own:
                        mm = nc.tensor.matmul(sbank[:], lhsT=ident_b[:], rhs=mkc[:, kt, :], start=False, stop=True)
                t_s = c.op("tensor", mm)
                pi, pbuf, pfree = pbufs.get()
                c.wait("scalar", t_s, pfree)
                scale = A_SCALE if kind == "A" else C_SCALE
                if (not own) and (G // 8 == m):
                    ins = nc.scalar.activation(out=pbuf[:], in_=sbank[:], func=AF.Exp, scale=scale,
                                               bias=vis[:, (G % 8):(G % 8) + 1])
                else:
                    ins = nc.scalar.activation(out=pbuf[:], in_=sbank[:], func=AF.Exp, scale=scale)
                t_p = c.op("scalar", ins)
                sring.release(si, t_p)
                pend.append((idx, G, kt, j, pi, pbuf, t_p))
                if len(pend) > LOOK:
                    emit_pv(pend.pop(0))
            while pend:
                emit_pv(pend.pop(0))
            tl = last_pe[0]
            c.wait("vector", tl, st["omix_free"])
            if kind == "A":
                vop(V.reciprocal(out=e1[:], in_=accb[2][:]))
                vop(V.tensor_tensor(out=e1[:], in0=accb[0][:], in1=e1[:], op=ALU.mult))
                vop(V.reciprocal(out=e2[:], in_=accb[3][:]))
                t = vop(V.tensor_tensor(out=e2[:], in0=accb[1][:], in1=e2[:], op=ALU.mult))
                st["acc_free"] = t
                t = vop(V.scalar_tensor_tensor(out=e1[:], in0=e2[:], scalar=neglam[:, 0:1], in1=e1[:], op0=ALU.mult,
                                               op1=ALU.add))
                c.wait("scalar", t)
                t = aop(nc.scalar.activation(out=e2[:], in_=e1[:], func=AF.Square))
                c.wait("tensor", t, st["misc_free"])
                tmm = c.op("tensor", nc.tensor.matmul(misc[:], lhsT=ones128[:], rhs=e2[:], start=True, stop=True))
                c.wait("scalar", tmm)
                t = aop(nc.scalar.activation(out=e2[:], in_=misc[:], func=AF.Sqrt, bias=eps_rms[:], scale=1.0))
                st["misc_free"] = t
                c.wait("vector", t)
                vop(V.reciprocal(out=e2[:], in_=e2[:]))
                t = vop(V.tensor_tensor(out=e1[:], in0=e1[:], in1=e2[:], op=ALU.mult))
                c.wait("scalar", t)
                aop(nc.scalar.activation(out=omix[:, h, :], in_=e1[:], func=AF.Copy, scale=gA_col[:, 0:1]))
                c.wait("vector", ("e_scalar", c.cnt["e_scalar"]))
            else:
                vop(V.reciprocal(out=e1[:], in_=accb[2][:]))
                t = vop(V.tensor_tensor(out=omix[:, 10 + h, :], in0=accb[0][:], in1=e1[:], op=ALU.mult))
                st["acc_free"] = t
        st["q_free"] = ("e_tensor", c.cnt["e_tensor"])

        for j in range(4):
            c.wait("sync", ("e_vector", c.cnt["e_vector"]))
            tu = c.dma("sync", ubuf[:, 2:TT + 2], u_i[m, j], "ub")
            tb = c.dma("sync", bbuf[:], bb_i[m, j], "ub")
            vop(V.tensor_copy(out=ubuf[:, 0:2], in_=uhalo[:, j, 2 * m:2 * m + 2]))
            c.wait("vector", tu, tb)
            vop(V.tensor_scalar(out=e1[:], in0=ubuf[:, 0:TT], scalar1=cw[:, 0, j:j + 1], scalar2=None, op0=ALU.mult))
            vop(V.scalar_tensor_tensor(out=e1[:], in0=ubuf[:, 1:TT + 1], scalar=cw[:, 1, j:j + 1], in1=e1[:], op0=ALU.mult,
                                       op1=ALU.add))
            vop(V.scalar_tensor_tensor(out=e1[:], in0=ubuf[:, 2:TT + 2], scalar=cw[:, 2, j:j + 1], in1=e1[:], op0=ALU.mult,
                                       op1=ALU.add))
            vop(V.tensor_tensor(out=omix[:, 6 + j, :], in0=e1[:], in1=bbuf[:], op=ALU.mult))
        t_omix = ("e_vector", c.cnt["e_vector"])

        c.wait("sync", st["pre_free"])
        tres = c.dma("sync", pre[:], h32_i[m].rearrange("k p t -> p k t"), "res")
        c.wait("tensor", t_omix)
        c.wait("vector", tres)
        tlast = None
        for og in range(4):
            wi, wt, wfree = wring.get()
            c.wait("gpsimd", wfree)
            tw = c.dma("gpsimd", wt[:], w_fm[:, :, og * 512:(og + 1) * 512], f"w{wi}")
            c.wait("tensor", tw)
            for j in range(4):
                oc = og * 4 + j
                bi, bank, bfree = sring.get()
                c.wait("tensor", bfree)
                for k in range(KC):
                    mm = nc.tensor.matmul(bank[:], lhsT=wt[:, k, j * 128:(j + 1) * 128], rhs=omix[:, k, :],
                                          start=(k == 0), stop=(k == KC - 1))
                tmm = c.op("tensor", mm)
                c.wait("vector", tmm)
                tlast = vop(V.scalar_tensor_tensor(out=pre[:, oc, :], in0=pre[:, oc, :], scalar=ALPHA, in1=bank[:],
                                                   op0=ALU.mult, op1=ALU.add))
                sring.release(bi, tlast)
            wring.release(wi, tmm)
        st["omix_free"] = tmm
        done, psfree = emit_ln_fm(c, pre, tlast, lng, lnb, onesD, ln_sq, accb[0], accb[1], [st["acc_free"]], ln_stat, pre,
                                  h1b_sb, out_free=st["h1b_free"])
        st["acc_free"] = psfree
        c.wait("sync", done)
        c.dma("sync", h1_32[m].rearrange("k p t -> p k t"), pre[:], "out")
        c.dma("sync", h1b[m].rearrange("k p t -> p k t"), h1b_sb[:], "out")
        for t_ in range(2):
            to = c.dma("sync", h1h[m, t_].rearrange("(k p) -> p k", p=128), h1b_sb[:, :, TT - 2 + t_], "out",
                       allow_slow_non_contiguous=True)
        st["pre_free"] = to
        st["h1b_free"] = to

    for key in list(c.sem.keys()):
        if key.startswith("d_"):
            c.wait("sync", (key, c.cnt[key]))
    c.es.close()
    return nc


def build_p2b(layer, slot_list=None):
    c = Ctx()
    nc = c.nc
    V = nc.vector
    EI, EO = "ExternalInput", "ExternalOutput"
    last = (layer == DEPTH - 1)
    slots = list(range(NSLOT)) if slot_list is None else slot_list
    h1b = c.dram("h1b", [NSLOT, KC, 128, TT], BF16, EI)
    h1_32 = c.dram("h1_32", [NSLOT, KC, 128, TT], F32, EI)
    h1h_all = c.dram("h1h_all", [64, D], BF16, EI)
    selb_d = c.dram("selb", [64, 8], BF16, EI)
    w_gate = c.dram("w_gate", [D, D_FF], F32, EI)
    w_up = c.dram("w_up", [D, D_FF], F32, EI)
    w_down = c.dram("w_down", [D_FF, D], F32, EI)
    fconv = c.dram("fconv", [128, 3, FFC], F32, EI)
    ln_g = c.dram("ln_g", [D], F32, EI)
    ln_b = c.dram("ln_b", [D], F32, EI)
    onesD_d = c.dram("onesD", [128, 128], F32, EI)
    ident_f = c.dram("ident_f", [128, 128], F32, EI)
    if last:
        out_d = c.dram("out", [NSLOT, 4, 128, D], F32, EO)
    else:
        hb_o = c.dram("hb", [NSLOT, KC, 128, TT], BF16, EO)
        h32_o = c.dram("h32", [NSLOT, KC, 128, TT], F32, EO)

    h1T = c.sb([128, KC, TT], BF16, "h1T")
    hid = c.sb([128, FFC, TT], BF16, "hid")
    pre = c.sb([128, KC, TT], F32, "pre")
    gring = Ring([c.sb([128, KC, 256], BF16, f"wg{i}") for i in range(2)])
    uring = Ring([c.sb([128, KC, 256], BF16, f"wu{i}") for i in range(2)])
    dring = Ring([c.sb([128, 11, 512], BF16, f"wd{i}") for i in range(2)])
    onesD = c.sb([128, 128], F32, "onesD")
    identf = c.sb([128, 128], F32, "identf")
    c.eps_ln = c.sb([128, 1], F32, "eps_ln")
    lng = c.sb([128, KC], F32, "lng")
    lnb = c.sb([128, KC], F32, "lnb")
    fcw = c.sb([128, 3, FFC], F32, "fcw")
    h1h_sb = c.sb([64, D], BF16, "h1h_sb")
    selb = c.sb([64, 8], BF16, "selb")
    hh = c.sb([128, KC, 8], BF16, "hh")
    gbuf = Ring([c.sb([128, TT + 2], F32, f"gbuf{i}") for i in range(2)])
    ybuf = Ring([c.sb([128, TT], F32, f"ybuf{i}") for i in range(2)])
    e3 = c.sb([128, TT], F32, "e3")
    e4 = c.sb([128, TT], F32, "e4")
    ln_sq = Ring([c.sb([128, TT], F32, f"lnsq{i}") for i in range(2)])
    ln_stat = (c.sb([128, TT], F32, "ln_mean"), c.sb([128, TT], F32, "ln_msq"), c.sb([128, TT], F32, "ln_rstd"),
               Ring([e3, e4]))
    if last:
        xo = Ring([c.sb([128, D], F32, f"xo{i}") for i in range(2)])
    banks = [c.ps([128, TT], F32, f"bank{i}") for i in range(8)]
    pring = Ring(banks[0:6])
    bh_ring = Ring(banks[6:8])

    def vop(inst):
        t = c.op("vector", inst)
        c.wait("vector", t)
        return t

    tcs = [c.dma("sync", onesD[:], onesD_d, "const"), c.dma("sync", identf[:], ident_f, "const"),
           load_cols(c, lng[:], ln_g, "const"), load_cols(c, lnb[:], ln_b, "const"),
           c.dma("sync", fcw[:], fconv, "const"),
           c.dma("sync", h1h_sb[:], h1h_all, "const"), c.dma("sync", selb[:], selb_d, "const")]
    tcs = [tcs[-1]]
    t_ms = c.op("vector", V.memset(c.eps_ln[:], LN_EPS))
    for e in ("vector", "scalar", "tensor", "gpsimd"):
        c.wait(e, tcs, t_ms)
    for k in range(KC):
        mm = nc.tensor.matmul(banks[7][:, k * 8:(k + 1) * 8], lhsT=h1h_sb[0:64, k * 128:(k + 1) * 128], rhs=selb[0:64, :],
                              start=True, stop=True)
    tmm = c.op("tensor", mm)
    c.wait("vector", tmm)
    t0_ = vop(V.tensor_copy(out=hh[:].rearrange("p k e -> p (k e)"), in_=banks[7][:, 0:128]))
    bh_ring.release(1, t0_)
    c.wait("tensor", t0_)

    wg_fm = w_gate.rearrange("(k p) n -> p k n", p=128)
    wu_fm = w_up.rearrange("(k p) n -> p k n", p=128)
    wd_fm = w_down.rearrange("(k p) n -> p k n", p=128)
    st = {"h1T_free": None, "pre_free": None, "hid_free": None}

    for m in slots:
        c.wait("sync", st["h1T_free"])
        th = c.dma("sync", h1T[:], h1b[m].rearrange("k p t -> p k t"), "h1T")
        c.wait("tensor", th)
        for fg in range(22):
            gi, wg, gfree = gring.get()
            ui, wu, ufree = uring.get()
            c.wait("gpsimd", gfree, ufree)
            tg = c.dma("gpsimd", wg[:], wg_fm[:, :, fg * 256:(fg + 1) * 256], f"wg{gi}")
            tu = c.dma("gpsimd", wu[:], wu_fm[:, :, fg * 256:(fg + 1) * 256], f"wu{ui}")
            c.wait("tensor", tg, tu)
            for j in range(2):
                ffc = fg * 2 + j
                b1, bank_g, f1 = pring.get()
                b2, bank_u, f2 = pring.get()
                b3, bank_h, f3 = bh_ring.get()
                c.wait("tensor", f1, f2, f3)
                for k in range(KC):
                    nc.tensor.matmul(bank_g[:], lhsT=wg[:, k, j * 128:(j + 1) * 128], rhs=h1T[:, k, :], start=(k == 0),
                                     stop=(k == KC - 1))
                for k in range(KC):
                    nc.tensor.matmul(bank_h[:, 0:2], lhsT=wg[:, k, j * 128:(j + 1) * 128], rhs=hh[:, k, 2 * m:2 * m + 2],
                                     start=(k == 0), stop=(k == KC - 1))
                for k in range(KC):
                    mm = nc.tensor.matmul(bank_u[:], lhsT=wu[:, k, j * 128:(j + 1) * 128], rhs=h1T[:, k, :],
                                          start=(k == 0), stop=(k == KC - 1))
                tmm = c.op("tensor", mm)
                gbi, gb, gbf = gbuf.get()
                c.wait("scalar", tmm, gbf)
                c.op("scalar", nc.scalar.copy(out=gb[:, 2:TT + 2], in_=bank_g[:]))
                ta = c.op("scalar", nc.scalar.copy(out=gb[:, 0:2], in_=bank_h[:, 0:2]))
                pring.release(b1, ta)
                bh_ring.release(b3, ta)
                yi, yb, yf = ybuf.get()
                c.wait("vector", ta, yf)
                vop(V.tensor_scalar(out=yb[:], in0=gb[:, 0:TT], scalar1=fcw[:, 0, ffc:ffc + 1], scalar2=None, op0=ALU.mult))
                vop(V.scalar_tensor_tensor(out=yb[:], in0=gb[:, 1:TT + 1], scalar=fcw[:, 1, ffc:ffc + 1], in1=yb[:],
                                           op0=ALU.mult, op1=ALU.add))
                tv = vop(V.scalar_tensor_tensor(out=yb[:], in0=gb[:, 2:TT + 2], scalar=fcw[:, 2, ffc:ffc + 1], in1=yb[:],
                                                op0=ALU.mult, op1=ALU.add))
                gbuf.release(gbi, tv)
                c.wait("scalar", tv)
                ts = c.op("scalar", nc.scalar.activation(out=yb[:], in_=yb[:], func=AF.Silu))
                c.wait("vector", ts, st["hid_free"])
                th_ = vop(V.tensor_tensor(out=hid[:, ffc, :], in0=bank_u[:], in1=yb[:], op=ALU.mult))
                ybuf.release(yi, th_)
                pring.release(b2, th_)
            gring.release(gi, tmm)
            uring.release(ui, tmm)
        st["h1T_free"] = tmm
        t_hid = ("e_vector", c.cnt["e_vector"])
        c.wait("sync", st["pre_free"])
        tres = c.dma("sync", pre[:], h1_32[m].rearrange("k p t -> p k t"), "res")
        c.wait("tensor", t_hid)
        c.wait("vector", tres)
        tlast = None
        for og in range(4):
            accs = [pring.get() for _ in range(4)]
            c.wait("tensor", [a[2] for a in accs])
            for q in range(4):
                di, wd, dfree = dring.get()
                c.wait("gpsimd", dfree)
                td = c.dma("gpsimd", wd[:], wd_fm[:, q * 11:(q + 1) * 11, og * 512:(og + 1) * 512], f"wd{di}")
                c.wait("tensor", td)
                for f in range(11):
                    ffc = q * 11 + f
                    for j in range(4):
                        mm = nc.tensor.matmul(accs[j][1][:], lhsT=wd[:, f, j * 128:(j + 1) * 128], rhs=hid[:, ffc, :],
                                              start=(ffc == 0), stop=(ffc == FFC - 1))
                tmm = c.op("tensor", mm)
                dring.release(di, tmm)
            c.wait("vector", tmm)
            for j in range(4):
                oc = og * 4 + j
                tlast = vop(V.scalar_tensor_tensor(out=pre[:, oc, :], in0=pre[:, oc, :], scalar=ALPHA, in1=accs[j][1][:],
                                                   op0=ALU.mult, op1=ALU.add))
                pring.release(accs[j][0], tlast)
        st["hid_free"] = tmm
        b1, pm, f1 = pring.get()
        b2, pe2, f2 = pring.get()
        done, psfree = emit_ln_fm(c, pre, tlast, lng, lnb, onesD, ln_sq, pm, pe2, [f1, f2], ln_stat, pre, h1T,
                                  out_free=st["h1T_free"])
        pring.release(b1, psfree)
        pring.release(b2, psfree)
        if not last:
            c.wait("sync", done)
            c.dma("sync", h32_o[m].rearrange("k p t -> p k t"), pre[:], "out")
            to = c.dma("sync", hb_o[m].rearrange("k p t -> p k t"), h1T[:], "out")
            st["pre_free"] = to
            st["h1T_free"] = to
        else:
            c.wait("tensor", done)
            tmm = None
            for tt in range(4):
                xi, xb, xf = xo.get()
                for q in range(4):
                    bi, bank, bfree = pring.get()
                    c.wait("tensor", bfree)
                    for j in range(4):
                        k = 4 * q + j
                        mm = nc.tensor.transpose(bank[:, j * 128:(j + 1) * 128], pre[:, k, tt * 128:(tt + 1) * 128],
                                                 identf[:])
                    tmm = c.op("tensor", mm)
                    c.wait("scalar", tmm, xf)
                    tcp = c.op("scalar", nc.scalar.copy(out=xb[:, q * 512:(q + 1) * 512], in_=bank[:]))
                    pring.release(bi, tcp)
                c.wait("sync", tcp)
                to = c.dma("sync", out_d[m, tt], xb[:], f"xo{xi}")
                xo.release(xi, to)
            st["pre_free"] = tmm
            st["h1T_free"] = tmm

    for key in list(c.sem.keys()):
        if key.startswith("d_"):
            c.wait("sync", (key, c.cnt[key]))
    c.es.close()
    return nc


def _rope_consts():
    inv_freq = (10000.0 ** (-np.arange(0, 32, 2, dtype=np.float32) / 32)).astype(np.float32)
    g = inv_freq.astype(np.float64) / (2 * math.pi)
    G = np.mod(128.0 * g, 1.0)
    t = np.zeros((128, 4), np.float32)
    for base in (0, 64):
        for r in range(32):
            j = r % 16
            t[base + r, 0] = G[j]
            t[base + r, 1] = g[j]
            t[base + r, 2] = -1.0 if r < 16 else 1.0
    return t


def _attn_consts(i):
    k = np.arange(128)[:, None]
    q = np.arange(512)[None, :]
    cma = np.zeros((4, 128, 512), np.float32)
    mkc = np.zeros((4, 128, 512), np.float32)
    for kt in range(4):
        kk = 128 * kt + k
        masked = (kk // 64) > (q // 64)
        cma[kt] = -2.0 * np.maximum(kk - q, 0) - 1e6 * masked
        mkc[kt] = -60000.0 * masked
    sigI = np.stack([np.eye(128, dtype=np.float32) * (SLOPES[h] / A_SCALE) for h in range(6)])
    vis = np.zeros((128, 8), np.float32)
    for ip in range(8):
        vis[:, ip] = 0.0 if ip < i else NEGBIG
    sel = np.zeros((64, 8), np.float32)
    for m in range(4):
        for t in range(2):
            if i > 0:
                sel[(i - 1) * 8 + m * 2 + t, 2 * m + t] = 1.0
            elif m > 0:
                sel[7 * 8 + (m - 1) * 2 + t, 2 * m + t] = 1.0
    return dict(cma=cma.astype(NPBF), mkc=mkc.astype(NPBF), sigI=sigI.astype(NPBF), vis=vis, sel=sel)


def _run(nc, maps):
    res = run_bass_kernel_spmd(nc, maps, core_ids=list(range(NCORE)))
    return res.results


def kernel(**inp):
    inp = {k: np.asarray(v) for k, v in inp.items()}
    x = inp["x"][0]
    pos = inp["positions"][0].astype(np.int32)
    toks = [np.concatenate([np.arange(512 * (8 * m + i), 512 * (8 * m + i) + 512) for m in range(NSLOT)])
            for i in range(NCORE)]
    perm = np.concatenate([np.arange(16, 32), np.arange(16)])
    ident_f = np.eye(128, dtype=np.float32)
    onesD = np.full((128, 128), 1.0 / D, np.float32)
    ones128 = np.full((128, 128), 1.0 / 128, np.float32)
    ones_b = np.ones((128, 128), NPBF)
    ident_b = np.eye(128).astype(NPBF)
    ropec = _rope_consts()
    sig = (np.array(SLOPES, np.float32) / A_SCALE).reshape(6, 1).astype(np.float32)
    acs = [_attn_consts(i) for i in range(NCORE)]
    hb = None
    h32 = None
    out = np.zeros((S, D), np.float32)
    for L in range(DEPTH):
        w_in = np.ascontiguousarray(inp["w_in"][L])
        w_kr2 = np.ascontiguousarray(np.concatenate([w_in[:, 4608:4640], w_in[:, 4608:4640][:, perm]], 1))
        w_uq = np.ascontiguousarray(inp["w_uq"][L])
        w_uq_sw = w_uq.reshape(512, 6, 96).copy()
        w_uq_sw[:, :, 64:96] = w_uq_sw[:, :, 64:96][:, :, perm]
        w_uq_sw = np.ascontiguousarray(w_uq_sw.reshape(512, 576))
        wk = inp["w_ukv"][L].reshape(256, 6, 192)
        w_ukv = np.ascontiguousarray(np.concatenate([wk[:, :, :64].reshape(256, 384), wk[:, :, 64:].reshape(256, 768)], 1))
        common = dict(w_in=w_in, w_kr2=w_kr2, w_uq=w_uq, w_uq_sw=w_uq_sw, w_ukv=w_ukv, gq=inp["mla_q_norm_g"][L],
                      gkv=inp["mla_kv_norm_g"][L], ident_f=ident_f, onesD=onesD, ones_b=ones_b, ropec=ropec, sig=sig)
        maps = []
        for i in range(NCORE):
            d = dict(common)
            d["pos"] = np.ascontiguousarray(pos[toks[i]].reshape(NSLOT, TT))
            if L == 0:
                d["x_own"] = np.ascontiguousarray(x[toks[i]].reshape(NSLOT, 4, 128, D))
                d["ln_g"] = inp["ln_in_g"]
                d["ln_b"] = inp["ln_in_b"]
            else:
                d["hb"] = hb[i]
            maps.append(d)
        r1 = _run(build_p1(L), maps)
        if L == 0:
            h32 = [r1[i]["h32"] for i in range(NCORE)]
        allg = {k + "_all": np.stack([r1[i][k] for i in range(NCORE)]) for k in ("ka", "kc", "va", "vc")}
        uh_all = np.stack([r1[i]["uh"] for i in range(NCORE)]).reshape(64, 512)
        common = dict(w_o=np.ascontiguousarray(inp["w_o"][L]), conv_w=np.ascontiguousarray(inp["conv_w"][L]),
                      dlam=np.ascontiguousarray(inp["diff_lambda"][L]), gA=np.ascontiguousarray(inp["diff_norm_g"][L]),
                      ln_g=np.ascontiguousarray(inp["ln1_g"][L]), ln_b=np.ascontiguousarray(inp["ln1_b"][L]),
                      onesD=onesD, ones128=ones128, ones_b=ones_b, ident_b=ident_b, uh_all=uh_all, **allg)
        maps = []
        for i in range(NCORE):
            d = dict(common)
            d.update(acs[i])
            for k in ("qa", "qc", "ka", "kc", "va", "vc", "u", "bb"):
                d[k] = r1[i][k]
            d["h32"] = h32[i]
            maps.append(d)
        r2 = _run(build_p2a(L), maps)
        h1h_all = np.stack([r2[i]["h1h"] for i in range(NCORE)]).reshape(64, D)
        common = dict(w_gate=np.ascontiguousarray(inp["ffn_w_gate"][L]), w_up=np.ascontiguousarray(inp["ffn_w_up"][L]),
                      w_down=np.ascontiguousarray(inp["ffn_w_down"][L]),
                      fconv=np.ascontiguousarray(inp["ffn_conv_w"][L].reshape(3, FFC, 128).transpose(2, 0, 1)),
                      ln_g=np.ascontiguousarray(inp["ln2_g"][L]), ln_b=np.ascontiguousarray(inp["ln2_b"][L]),
                      onesD=onesD, ident_f=ident_f, h1h_all=h1h_all)
        maps = []
        for i in range(NCORE):
            d = dict(common)
            d["selb"] = acs[i]["sel"].astype(NPBF)
            d["h1b"] = r2[i]["h1b"]
            d["h1_32"] = r2[i]["h1_32"]
            maps.append(d)
        r3 = _run(build_p2b(L), maps)
        if L < DEPTH - 1:
            hb = [r3[i]["hb"] for i in range(NCORE)]
            h32 = [r3[i]["h32"] for i in range(NCORE)]
        else:
            for i in range(NCORE):
                out[toks[i]] = r3[i]["out"].reshape(NSLOT * TT, D)
    return out.reshape(1, S, D)
```
